# Optimizing a Trainium2 kernel written in Bass

```python
import math
import jax, jax.numpy as jnp
from jax import lax
import numpy as np

D_MODEL = 2048
BATCH = 4
SEQ = 4096
DEPTH = 4

GRID_W = 64
CTX_LEN = 256
N_MIXERS = 3
DA_HEADS = 16
DA_HEAD_DIM = 64
DA_WIDTH = DA_HEADS * 2 * DA_HEAD_DIM
ROPE_BASE = 10000.0
ROPE_FREQS = DA_HEAD_DIM // 4
Q_BLOCK = 128
CF_WIDTH = D_MODEL
CF_KERNEL = 31
SC_WIDTH = D_MODEL
SC_KERNEL = 3
NORM_EPS = 1e-6
LN_EPS = 1e-5

kernel_name = "hybrid_diffattn_conformer_shortconv_dit"


def rmsnorm(x, g):
    xf = x.astype(jnp.float32)
    y = xf * lax.rsqrt(jnp.mean(xf * xf, axis=-1, keepdims=True) + NORM_EPS)
    return (y * g.astype(jnp.float32)).astype(x.dtype)


def layernorm(x, g, b):
    xf = x.astype(jnp.float32)
    mu = jnp.mean(xf, axis=-1, keepdims=True)
    var = jnp.mean(jnp.square(xf - mu), axis=-1, keepdims=True)
    y = (xf - mu) * lax.rsqrt(var + LN_EPS)
    return (y * g.astype(jnp.float32) + b.astype(jnp.float32)).astype(x.dtype)


def adaln(cond, w, b):
    m = jax.nn.silu(cond) @ w + b
    return jnp.split(m, 3, axis=-1)


def dwconv(x, w):
    k = w.shape[0]
    return lax.conv_general_dilated(
        x, w[:, None, :].astype(x.dtype), window_strides=(1,),
        padding=[(k // 2, k // 2)], dimension_numbers=("NWC", "WIO", "NWC"),
        feature_group_count=x.shape[-1])


def axial_rope_tables(rows):
    row = jnp.repeat(jnp.arange(rows, dtype=jnp.float32), GRID_W)
    col = jnp.tile(jnp.arange(GRID_W, dtype=jnp.float32), rows)
    inv = ROPE_BASE ** (-jnp.arange(ROPE_FREQS, dtype=jnp.float32) / ROPE_FREQS)
    ang = jnp.stack([row[:, None] * inv, col[:, None] * inv], axis=1)
    return jnp.cos(ang), jnp.sin(ang)


def apply_rope(x, cos, sin):
    xs = x.reshape(x.shape[:-1] + (2, 2, ROPE_FREQS))
    x1, x2 = xs[..., 0, :], xs[..., 1, :]
    c = cos[:, None, None].astype(x.dtype)
    s = sin[:, None, None].astype(x.dtype)
    out = jnp.stack([x1 * c - x2 * s, x2 * c + x1 * s], axis=-2)
    return out.reshape(x.shape)


def diff_attention(hl, hc, p, layer_idx, cos, sin, keep_ctx):
    B, S, _ = hl.shape
    L = hc.shape[1]
    H, dh = DA_HEADS, DA_HEAD_DIM
    scale = 1.0 / math.sqrt(dh)
    lam_init = 0.8 - 0.6 * math.exp(-0.3 * layer_idx)
    f32 = jnp.float32
    lam = (jnp.exp(jnp.sum(p["lam_q1"].astype(f32) * p["lam_k1"].astype(f32)))
           - jnp.exp(jnp.sum(p["lam_q2"].astype(f32) * p["lam_k2"].astype(f32))) + lam_init)
    w_in = p["w_in"]

    ql, kl, vl, zl = jnp.split(hl @ w_in, 4, axis=-1)
    ql = apply_rope(ql.reshape(B, S, H, 2, dh), cos, sin)
    kl = apply_rope(kl.reshape(B, S, H, 2, dh), cos, sin)
    vl = vl.reshape(B, S, H, 2 * dh)

    if keep_ctx:
        qc, kc, vc, zc = jnp.split(hc @ w_in, 4, axis=-1)
        qc = qc.reshape(B, L, H, 2, dh)
    else:
        kc, vc = jnp.split(hc @ w_in[:, DA_WIDTH:3 * DA_WIDTH], 2, axis=-1)
    kc = kc.reshape(B, L, H, 2, dh)
    vc = vc.reshape(B, L, H, 2 * dh)

    k_all = jnp.concatenate([kc, kl], axis=1)
    v_all = jnp.concatenate([vc, vl], axis=1)

    def attend(q, k, v):
        s = jnp.einsum("bqhcd,bkhcd->bhcqk", q, k).astype(f32) * scale
        pr = jax.nn.softmax(s, axis=-1).astype(v.dtype)
        o = jnp.einsum("bhcqk,bkhe->bqhce", pr, v)
        return o[..., 0, :] - lam.astype(o.dtype) * o[..., 1, :]

    def finish(o, z):
        n = o.shape[1]
        on = rmsnorm(o, p["head_g"]) * (1.0 - lam_init)
        return (on.reshape(B, n, DA_WIDTH) * jax.nn.silu(z)) @ p["w_out"]

    nb = S // Q_BLOCK
    qb = ql.reshape(B, nb, Q_BLOCK, H, 2, dh).transpose(1, 0, 2, 3, 4, 5)
    ol = lax.map(lambda q: attend(q, k_all, v_all), qb)
    ol = ol.transpose(1, 0, 2, 3, 4).reshape(B, S, H, 2 * dh)
    yl = finish(ol, zl)
    yc = finish(attend(qc, kc, vc), zc) if keep_ctx else None
    return yl, yc


def conformer_conv(h, p):
    a, b, z = jnp.split(h @ p["w_in"], 3, axis=-1)
    u = a * jax.nn.sigmoid(b)
    u = dwconv(u, p["dw_w"]) + p["dw_b"]
    u = jax.nn.silu(layernorm(u, p["ln_g"], p["ln_b"]))
    return (u * jax.nn.silu(z)) @ p["w_out"]


def short_conv(h, p):
    bg, cg, v, z = jnp.split(h @ p["w_in"], 4, axis=-1)
    y = bg * dwconv(cg * v, p["conv_w"])
    return (y * jax.nn.silu(z)) @ p["w_out"]


def setup_inputs(seed: int = 0) -> dict:
    key = jax.random.key(seed)
    ks = iter(jax.random.split(key, 96))
    d = D_MODEL

    def nrm(shape, s):
        return jax.random.normal(next(ks), shape, jnp.float32) * s

    inp = {}
    inp["x"] = nrm((BATCH, SEQ, d), 1.0)
    inp["c"] = nrm((BATCH, d), 1.0)
    inp["ctx"] = nrm((BATCH, CTX_LEN, d), 1.0)
    inp["c_ctx"] = nrm((d,), 1.0)

    def common(pre):
        inp[pre + "norm_g"] = 1.0 + nrm((d,), 0.02)
        inp[pre + "ada_w"] = nrm((d, 3 * d), 0.5 * d ** -0.5)
        inp[pre + "ada_b"] = nrm((3 * d,), 0.02)

    def attn(pre):
        common(pre)
        inp[pre + "w_in"] = nrm((d, 4 * DA_WIDTH), d ** -0.5)
        inp[pre + "lam_q1"] = nrm((DA_HEAD_DIM,), 0.1)
        inp[pre + "lam_k1"] = nrm((DA_HEAD_DIM,), 0.1)
        inp[pre + "lam_q2"] = nrm((DA_HEAD_DIM,), 0.1)
        inp[pre + "lam_k2"] = nrm((DA_HEAD_DIM,), 0.1)
        inp[pre + "head_g"] = 1.0 + nrm((2 * DA_HEAD_DIM,), 0.02)
        inp[pre + "w_out"] = nrm((DA_WIDTH, d), DA_WIDTH ** -0.5)

    def conformer(pre):
        common(pre)
        inp[pre + "w_in"] = nrm((d, 3 * CF_WIDTH), d ** -0.5)
        inp[pre + "dw_w"] = nrm((CF_KERNEL, CF_WIDTH), CF_KERNEL ** -0.5)
        inp[pre + "dw_b"] = nrm((CF_WIDTH,), 0.02)
        inp[pre + "ln_g"] = 1.0 + nrm((CF_WIDTH,), 0.02)
        inp[pre + "ln_b"] = nrm((CF_WIDTH,), 0.02)
        inp[pre + "w_out"] = nrm((CF_WIDTH, d), CF_WIDTH ** -0.5)

    def shortconv(pre):
        common(pre)
        inp[pre + "w_in"] = nrm((d, 4 * SC_WIDTH), d ** -0.5)
        inp[pre + "conv_w"] = nrm((SC_KERNEL, SC_WIDTH), SC_KERNEL ** -0.5)
        inp[pre + "w_out"] = nrm((SC_WIDTH, d), SC_WIDTH ** -0.5)

    builders = (attn, conformer, shortconv)
    for i in range(DEPTH):
        builders[i % N_MIXERS]("l%d_" % i)
    inp["final_norm_g"] = 1.0 + nrm((d,), 0.02)
    return inp


def reference(x, c, ctx, c_ctx,
              l0_norm_g, l0_ada_w, l0_ada_b, l0_w_in, l0_lam_q1, l0_lam_k1, l0_lam_q2, l0_lam_k2, l0_head_g, l0_w_out,
              l1_norm_g, l1_ada_w, l1_ada_b, l1_w_in, l1_dw_w, l1_dw_b, l1_ln_g, l1_ln_b, l1_w_out,
              l2_norm_g, l2_ada_w, l2_ada_b, l2_w_in, l2_conv_w, l2_w_out,
              l3_norm_g, l3_ada_w, l3_ada_b, l3_w_in, l3_lam_q1, l3_lam_k1, l3_lam_q2, l3_lam_k2, l3_head_g, l3_w_out,
              final_norm_g):
    S = x.shape[1]
    rows = S // GRID_W
    cos, sin = axial_rope_tables(rows)

    kinds = ("attn", "conformer", "shortconv")
    layers = [
        dict(norm_g=l0_norm_g, ada_w=l0_ada_w, ada_b=l0_ada_b, w_in=l0_w_in, lam_q1=l0_lam_q1, lam_k1=l0_lam_k1,
             lam_q2=l0_lam_q2, lam_k2=l0_lam_k2, head_g=l0_head_g, w_out=l0_w_out),
        dict(norm_g=l1_norm_g, ada_w=l1_ada_w, ada_b=l1_ada_b, w_in=l1_w_in, dw_w=l1_dw_w, dw_b=l1_dw_b,
             ln_g=l1_ln_g, ln_b=l1_ln_b, w_out=l1_w_out),
        dict(norm_g=l2_norm_g, ada_w=l2_ada_w, ada_b=l2_ada_b, w_in=l2_w_in, conv_w=l2_conv_w, w_out=l2_w_out),
        dict(norm_g=l3_norm_g, ada_w=l3_ada_w, ada_b=l3_ada_b, w_in=l3_w_in, lam_q1=l3_lam_q1, lam_k1=l3_lam_k1,
             lam_q2=l3_lam_q2, lam_k2=l3_lam_k2, head_g=l3_head_g, w_out=l3_w_out),
    ]

    xl, xc = x, ctx
    for i in range(DEPTH):
        kind = kinds[i % N_MIXERS]
        p = layers[i]
        last = i == DEPTH - 1
        sh_l, sc_l, g_l = adaln(c, p["ada_w"], p["ada_b"])
        hl = rmsnorm(xl, p["norm_g"]) * (1.0 + sc_l[:, None]) + sh_l[:, None]
        need_ctx = (not last) or kind == "attn"
        if need_ctx:
            sh_c, sc_c, g_c = adaln(c_ctx, p["ada_w"], p["ada_b"])
            hc = rmsnorm(xc, p["norm_g"]) * (1.0 + sc_c) + sh_c
        if kind == "attn":
            yl, yc = diff_attention(hl, hc, p, i, cos, sin, keep_ctx=not last)
        elif kind == "conformer":
            yl = conformer_conv(hl, p)
            yc = conformer_conv(hc, p) if not last else None
        else:
            yl = short_conv(hl, p)
            yc = short_conv(hc, p) if not last else None
        xl = xl + g_l[:, None] * yl
        if not last:
            xc = xc + g_c * yc
    return rmsnorm(xl, final_norm_g)
```

```python
import contextlib
import math
import numpy as np
import ml_dtypes
import concourse.bass as bass
import concourse.mybir as mybir
from concourse.bass_utils import run_bass_kernel_spmd

F32 = mybir.dt.float32
BF16 = mybir.dt.bfloat16
ALU = mybir.AluOpType
AF = mybir.ActivationFunctionType

D = 2048
NTOK = 2048
NCTX = 256
T = NTOK + NCTX
NCORES = 8
DEPTH = 4
KINDS = ("attn", "conf", "sconv", "attn")
NCOLS = (8192, 6144, 8192, 8192)
NKEYS = NCTX + 2 * NTOK
HAL = 15
TP = NTOK + 2 * HAL + NCTX + 2 * HAL
CTXP = NTOK + 2 * HAL
TT = [(0, 512), (512, 512), (1024, 512), (1536, 512), (2048, 256)]
NORM_EPS = 1e-6
LN_EPS = 1e-5

COMPUTE = ("pe", "act", "dve", "pool")
NDMASEM = 8


class Op:
    __slots__ = ("eng", "fn", "deps", "needed", "ev", "is_dma", "qi")

    def __init__(self, eng, fn, is_dma):
        self.eng = eng
        self.fn = fn
        self.deps = []
        self.needed = False
        self.ev = None
        self.is_dma = is_dma
        self.qi = None


class Prog:
    def __init__(self, nc):
        self.nc = nc
        self.streams = {e: [] for e in ("pe", "act", "dve", "pool", "sp")}
        self.lastw = {}
        self.readers = {}
        self.ndma = {"sp": 0, "pool": 0}
        self.dma_ops = {"sp": [], "pool": []}
        self.stack = contextlib.ExitStack()
        self.out_dma_ops = []
        self.pending = {}
        self.uid = 0

    def sbuf(self, name, shape, dt, stack=None):
        self.uid += 1
        return (stack or self.stack).enter_context(self.nc.sbuf_tensor("%s_%d" % (name, self.uid), list(shape), dt))

    def psum(self, name, shape, dt, stack=None):
        self.uid += 1
        return (stack or self.stack).enter_context(self.nc.psum_tensor("%s_%d" % (name, self.uid), list(shape), dt))

    def _track(self, op, r, w):
        deps = set()
        for k in r:
            lw = self.lastw.get(k)
            if lw is not None:
                deps.add(lw)
        for k in w:
            lw = self.lastw.get(k)
            if lw is not None:
                deps.add(lw)
            for rd in self.readers.get(k, ()):
                deps.add(rd)
        deps.discard(op)
        for d in deps:
            if d.eng == op.eng and not d.is_dma and not op.is_dma:
                if op.eng == "pe":
                    continue
                if not any(self.lastw.get(k) is d for k in r):
                    continue
            d.needed = True
            op.deps.append(d)
        pend = self.pending.pop(op.eng, None)
        if pend:
            for d in pend:
                if d is not op and not (d.eng == op.eng and not d.is_dma):
                    d.needed = True
                    op.deps.append(d)
        for k in w:
            self.lastw[k] = op
            self.readers[k] = []
        for k in r:
            self.readers.setdefault(k, []).append(op)

    def op(self, eng, fn, r=(), w=()):
        o = Op(eng, fn, False)
        self._track(o, r, w)
        self.streams[eng].append(o)
        return o

    def dma(self, fn, r=(), w=(), q="sp", is_out=False):
        o = Op(q, fn, True)
        o.needed = True
        i = self.ndma[q]
        self.ndma[q] += 1
        o.qi = i
        self._track(o, r, w)
        if i >= NDMASEM:
            o.deps.append(self.dma_ops[q][i - NDMASEM])
        self.dma_ops[q].append(o)
        self.streams[q].append(o)
        if is_out:
            self.out_dma_ops.append(o)
        return o

    def barrier(self):
        deps = []
        for e, ops in self.streams.items():
            comp = [o for o in ops if not o.is_dma]
            if comp:
                deps.append(comp[-1])
        for q in ("sp", "pool"):
            deps.extend(self.dma_ops[q][-NDMASEM:])
        for e in self.streams:
            self.pending[e] = list(deps) + self.pending.get(e, [])
        self.lastw = {}
        self.readers = {}

    def build(self):
        nc = self.nc
        st = self.stack
        sems = {}
        for e in COMPUTE:
            sems[e] = st.enter_context(nc.semaphore("s_" + e))
        for q in ("sp", "pool"):
            for j in range(NDMASEM):
                sems[(q, j)] = st.enter_context(nc.semaphore("d_%s%d" % (q, j)))
        for e, ops in self.streams.items():
            cnt = 0
            for o in ops:
                if o.is_dma:
                    o.ev = (sems[(o.eng, o.qi % NDMASEM)], 16 * (o.qi // NDMASEM + 1))
                elif o.needed:
                    cnt += 1
                    o.ev = (sems[e], cnt)
        engmap = {"pe": "tensor", "act": "scalar", "dve": "vector", "pool": "gpsimd", "sp": "sync"}
        block = st.enter_context(nc.Block())
        for e in ("sp", "pool", "act", "dve", "pe"):
            ops = self.streams[e]
            extra = self.out_dma_ops if e == "sp" else []

            def body(eng, ops=ops, extra=extra):
                waited = {}
                for o in ops:
                    for d in o.deps:
                        s, v = d.ev
                        if waited.get(id(s), 0) >= v:
                            continue
                        waited[id(s)] = v
                        eng.wait_ge(s, v)
                    ins = o.fn(eng)
                    if o.is_dma:
                        ins.then_inc(o.ev[0], 16)
                    elif o.needed:
                        ins.then_inc(o.ev[0], 1)
                for d in extra:
                    s, v = d.ev
                    if waited.get(id(s), 0) >= v:
                        continue
                    waited[id(s)] = v
                    eng.wait_ge(s, v)

            getattr(block, engmap[e])(body)
        st.close()


class Ctx:
    pass


def setup_consts(P, din):
    cx = Ctx()
    cx.P = P
    cx.ps = [P.psum("ps%d" % i, [128, 512], F32) for i in range(8)]
    cx.ident = P.sbuf("ident", [128, 128], F32)
    cx.onesf = P.sbuf("onesf", [128, 128], F32)
    cx.onesb = P.sbuf("onesb", [128, 128], BF16)
    cx.sel = P.sbuf("sel", [2, 256], F32)
    P.dma(lambda e: e.dma_start(out=cx.ident[:], in_=din["ident"]), w=["ident"])
    P.dma(lambda e: e.dma_start(out=cx.sel[:], in_=din["sel"]), w=["sel"])
    P.op("pool", lambda e: e.memset(cx.onesf[:], 1.0), w=["onesf"])
    P.op("pool", lambda e: e.memset(cx.onesb[:], 1.0), w=["onesb"])
    return cx


def emit_adaln(cx, st, ada_w, ada_b2, norm_g2, c2T, grow_d, wslots):
    P = cx.P
    ps = cx.ps
    craw = P.sbuf("craw", [128, 16, 2], F32, st)
    sc2 = P.sbuf("sc2", [128, 16, 2], BF16, st)
    mpart = P.sbuf("mpart", [2, 2048], F32, st)
    ab2 = P.sbuf("ab2", [2, 2048], F32, st)
    ng2 = P.sbuf("ng2", [2, 2048], F32, st)
    P.dma(lambda e: e.dma_start(out=craw[:], in_=c2T), w=["craw"])
    P.dma(lambda e: e.dma_start(out=ng2[:], in_=norm_g2), w=["ng2"])
    P.op("act", lambda e: e.activation(out=sc2[:], in_=craw[:], func=AF.Silu), r=["craw"], w=["sc2"])
    wv = ada_w.rearrange("(kc p) n -> p kc n", p=128)
    NW = len(wslots)
    out = {}
    bi = 0
    k = 0
    for pi, part in enumerate(("sh", "a", "g")):
        P.dma(lambda e, pi=pi: e.dma_start(out=ab2[:], in_=ada_b2[:, pi * 2048:(pi + 1) * 2048]), w=["ab2"])
        for nb in range(4):
            n = pi * 4 + nb
            slot = bi % NW
            wb = wslots[slot]
            P.dma(lambda e, wb=wb, n=n: e.dma_start(out=wb[:], in_=wv[:, :, n * 512:(n + 1) * 512]),
                  w=["wblk%d" % slot], q="pool")
            pb = ps[bi % 2]

            def mm(e, wb=wb, pb=pb):
                for kc in range(16):
                    ins = e.matmul(pb[0:2, :], lhsT=sc2[:, kc, :], rhs=wb[:, kc, :], start=(kc == 0), stop=(kc == 15))
                return ins
            P.op("pe", mm, r=["sc2", "wblk%d" % slot], w=["ps%d" % (bi % 2)])
            P.op("dve", lambda e, pb=pb, nb=nb: e.tensor_tensor(out=mpart[:, nb * 512:(nb + 1) * 512], in0=pb[0:2, :],
                                                                 in1=ab2[:, nb * 512:(nb + 1) * 512], op=ALU.add),
                 r=["ps%d" % (bi % 2), "ab2"], w=["mpart"])
            bi += 1
        if part == "a":
            P.op("dve", lambda e: e.scalar_tensor_tensor(out=mpart[:], in0=mpart[:], scalar=1.0, in1=ng2[:],
                                                          op0=ALU.add, op1=ALU.mult), r=["mpart", "ng2"], w=["mpart"])
        if part == "g":
            P.dma(lambda e: e.dma_start(out=grow_d, in_=mpart[:]), r=["mpart"])
            continue
        for row in (0, 1):
            bc = P.sbuf("bc_%s%d" % (part, row), [128, 2048], F32, st)
            for n in range(4):
                pb = ps[2 + k % 2]
                key = "ps%d" % (2 + k % 2)
                P.op("pe", lambda e, pb=pb, row=row, n=n: e.matmul(
                    pb[:], lhsT=cx.sel[0:2, row * 128:(row + 1) * 128],
                    rhs=mpart[0:2, n * 512:(n + 1) * 512], start=True, stop=True),
                    r=["mpart", "sel"], w=[key])
                P.op("act", lambda e, pb=pb, bc=bc, n=n: e.copy(out=bc[:, n * 512:(n + 1) * 512], in_=pb[:]),
                     r=[key], w=["bc_%s%d" % (part, row)])
                k += 1
            out["%s%d" % (part, row)] = bc
    return out


def emit_norm(cx, st, xl, hT, bc):
    P = cx.P
    ps = cx.ps
    NS = 2
    xt = [P.sbuf("xt%d" % i, [128, D], F32, st) for i in range(NS)]
    xm = [P.sbuf("xm%d" % i, [128, D], F32, st) for i in range(1)]
    junk = P.sbuf("junk", [128, D], BF16, st)
    ssq = P.sbuf("ssq", [128, 4], F32, st)
    rstd = P.sbuf("rstd", [128, 4], F32, st)
    ntile = T // 128
    for i in range(ntile):
        s = i % NS
        m = 0
        row = 0 if i < NTOK // 128 else 1
        a_bc, sh_bc = bc["a%d" % row], bc["sh%d" % row]
        akey, shkey = "bc_a%d" % row, "bc_sh%d" % row
        P.dma(lambda e, s=s, i=i: e.dma_start(out=xt[s][:], in_=xl[i * 128:(i + 1) * 128, :]), w=["xt%d" % s])
        c = i % 4
        P.op("act", lambda e, s=s, c=c: e.activation(out=junk[:], in_=xt[s][:], func=AF.Square, accum_out=ssq[:, c:c + 1]),
             r=["xt%d" % s], w=["junk", "ssq%d" % c])
        P.op("dve", lambda e, c=c: e.tensor_scalar(out=rstd[:, c:c + 1], in0=ssq[:, c:c + 1], scalar1=1.0 / D, scalar2=NORM_EPS,
                                                    op0=ALU.mult, op1=ALU.add), r=["ssq%d" % c], w=["rstd%d" % c])
        P.op("act", lambda e, c=c: e.sqrt(out=rstd[:, c:c + 1], in_=rstd[:, c:c + 1]), r=["rstd%d" % c], w=["rstd%d" % c])
        P.op("dve", lambda e, c=c: e.reciprocal(out=rstd[:, c:c + 1], in_=rstd[:, c:c + 1]), r=["rstd%d" % c], w=["rstd%d" % c])
        P.op("pool", lambda e, s=s, m=m, c=c: e.tensor_scalar(
            out=xm[m][:], in0=xt[s][:], scalar1=rstd[:, c:c + 1], scalar2=None, op0=ALU.mult),
            r=["xt%d" % s, "rstd%d" % c], w=["xm%d" % m])
        P.op("pool", lambda e, m=m, a_bc=a_bc: e.tensor_tensor(out=xm[m][:], in0=xm[m][:], in1=a_bc[:], op=ALU.mult),
             r=["xm%d" % m, akey], w=["xm%d" % m])
        P.op("dve", lambda e, m=m, sh_bc=sh_bc: e.tensor_tensor(out=xm[m][:], in0=xm[m][:], in1=sh_bc[:], op=ALU.add),
             r=["xm%d" % m, shkey], w=["xm%d" % m])
        for g in range(4):
            b = (i * 4 + g) % 8
            pb = ps[b]

            def tr(e, pb=pb, g=g, m=m):
                for j in range(4):
                    kc = g * 4 + j
                    ins = e.transpose(out=pb[:, j * 128:(j + 1) * 128], in_=xm[m][:, kc * 128:(kc + 1) * 128], identity=cx.ident[:])
                return ins
            P.op("pe", tr, r=["xm%d" % m, "ident"], w=["ps%d" % b])
            dst = hT[:, g * 4:(g + 1) * 4, i * 128:(i + 1) * 128]
            src = pb[:].rearrange("p (a b) -> p a b", a=4)
            if g % 2 == 0:
                P.op("act", lambda e, dst=dst, src=src: e.copy(out=dst, in_=src), r=["ps%d" % b], w=["hTw_%d_%d" % (i, g)])
            else:
                P.op("dve", lambda e, dst=dst, src=src: e.tensor_copy(out=dst, in_=src), r=["ps%d" % b], w=["hTw_%d_%d" % (i, g)])


def make_plan(kind):
    plan = []
    if kind == "attn":
        for sb in range(16):
            segs = [(sb * 512, 512)]
            if sb < 4:
                units = [("rope_q", [u], sb * 4 + u) for u in range(4)]
            elif sb < 8:
                units = [("rope_k", [u], (sb - 4) * 4 + u) for u in range(4)]
            elif sb < 12:
                units = [("vtok", None, (sb - 8) * 4)]
            else:
                units = [("silu", [u], (sb - 12) * 4 + u) for u in range(4)]
            plan.append((segs, units))
    elif kind == "conf":
        for j in range(8):
            segs = [(j * 256, 256), (2048 + j * 256, 256)]
            plan.append((segs, [("glu", [u, 2 + u], j * 2 + u) for u in range(2)]))
        for sb in range(4):
            plan.append(([(4096 + sb * 512, 512)], [("silu", [u], sb * 4 + u) for u in range(4)]))
    elif kind == "sconv":
        for j in range(8):
            segs = [(2048 + j * 256, 256), (4096 + j * 256, 256)]
            plan.append((segs, [("mul", [u, 2 + u], j * 2 + u) for u in range(2)]))
        for j in range(8):
            segs = [(j * 256, 256), (6144 + j * 256, 256)]
            plan.append((segs, [("mulsilu", [u, 2 + u], j * 2 + u) for u in range(2)]))
    return plan


def emit_inproj(cx, st, kind, w_in, hT, wslots, outs, rope=None):
    P = cx.P
    ps = cx.ps
    plan = make_plan(kind)
    NW = len(wslots)
    wv = w_in.rearrange("(kc p) n -> p kc n", p=128)
    stage32 = [P.sbuf("stg32_%d" % i, [128, T], F32, st) for i in range(2)] if kind != "attn" else None
    stage16 = [P.sbuf("stg16_%d" % i, [128, T], BF16, st) for i in range(2)]
    tmp32 = [P.sbuf("tmp32_%d" % i, [128, 512], F32, st) for i in range(3)]
    vst = [P.sbuf("vst%d" % i, [128, 512], BF16, st) for i in range(3)] if kind == "attn" else None
    if kind == "attn":
        t1 = [P.sbuf("ropet1_%d" % i, [128, 512], F32, st) for i in range(2)]
        t2 = [P.sbuf("ropet2_%d" % i, [128, 512], F32, st) for i in range(2)]
    cnt = {"ps": 0, "s32": 0, "s16": 0, "tmp": 0, "vst": 0, "rp": 0}

    def load_w(i):
        segs, _ = plan[i]
        slot = i % NW
        wb = wslots[slot]
        c0 = 0
        for (col0, nc_) in segs:
            P.dma(lambda e, wb=wb, c0=c0, col0=col0, nc_=nc_: e.dma_start(out=wb[:, :, c0:c0 + nc_], in_=wv[:, :, col0:col0 + nc_]),
                  w=["wblk%d" % slot], q="pool")
            c0 += nc_

    def mm_fm(pb, wb, lb, t0, n):
        def f(e):
            for kc in range(16):
                ins = e.matmul(pb[:, 0:n], lhsT=wb[:, kc, lb * 128:(lb + 1) * 128], rhs=hT[:, kc, t0:t0 + n],
                               start=(kc == 0), stop=(kc == 15))
            return ins
        return f

    for i in range(min(NW - 1, len(plan))):
        load_w(i)
    for i, (segs, units) in enumerate(plan):
        if i + NW - 1 < len(plan):
            load_w(i + NW - 1)
        slot = i % NW
        wb = wslots[slot]
        wkey = "wblk%d" % slot
        for (ut, lbs, oi) in units:
            if ut == "vtok":
                for ti in range(T // 128):
                    b = cnt["ps"] % 8
                    cnt["ps"] += 1
                    pb = ps[b]

                    def f(e, pb=pb, ti=ti, wb=wb):
                        for kc in range(16):
                            ins = e.matmul(pb[:], lhsT=hT[:, kc, ti * 128:(ti + 1) * 128], rhs=wb[:, kc, :],
                                           start=(kc == 0), stop=(kc == 15))
                        return ins
                    P.op("pe", f, r=[wkey, "hT"], w=["ps%d" % b])
                    vs = cnt["vst"] % 3
                    cnt["vst"] += 1
                    if ti % 2 == 0:
                        P.op("act", lambda e, pb=pb, vs=vs: e.copy(out=vst[vs][:], in_=pb[:]), r=["ps%d" % b], w=["vst%d" % vs])
                    else:
                        P.op("dve", lambda e, pb=pb, vs=vs: e.tensor_copy(out=vst[vs][:], in_=pb[:]), r=["ps%d" % b], w=["vst%d" % vs])
                    dst = outs["V"][oi:oi + 4, ti * 128:(ti + 1) * 128, :].rearrange("h t e -> t h e")
                    P.dma(lambda e, dst=dst, vs=vs: e.dma_start(out=dst, in_=vst[vs][:].rearrange("t (h e) -> t h e", h=4)),
                          r=["vst%d" % vs])
                continue
            if ut in ("glu", "mul"):
                sidx = cnt["s32"] % 2
                cnt["s32"] += 1
                stg, skey = stage32[sidx], "stg32_%d" % sidx
            else:
                sidx = cnt["s16"] % 2
                cnt["s16"] += 1
                stg, skey = stage16[sidx], "stg16_%d" % sidx
            for (t0, n) in TT:
                pbs = []
                for lb in lbs:
                    b = cnt["ps"] % 8
                    cnt["ps"] += 1
                    P.op("pe", mm_fm(ps[b], wb, lb, t0, n), r=[wkey, "hT"], w=["ps%d" % b])
                    pbs.append(b)
                dst = stg[:, t0:t0 + n]
                if ut == "silu":
                    b = pbs[0]
                    P.op("act", lambda e, dst=dst, b=b, n=n: e.activation(out=dst, in_=ps[b][:, 0:n], func=AF.Silu),
                         r=["ps%d" % b], w=[skey])
                elif ut in ("glu", "mul", "mulsilu"):
                    b0, b1 = pbs
                    tm = cnt["tmp"] % 3
                    cnt["tmp"] += 1
                    if ut == "glu":
                        P.op("act", lambda e, tm=tm, b1=b1, n=n: e.activation(out=tmp32[tm][:, 0:n], in_=ps[b1][:, 0:n], func=AF.Sigmoid),
                             r=["ps%d" % b1], w=["tmp32_%d" % tm])
                        other = b0
                    elif ut == "mul":
                        P.op("act", lambda e, tm=tm, b1=b1, n=n: e.copy(out=tmp32[tm][:, 0:n], in_=ps[b1][:, 0:n]),
                             r=["ps%d" % b1], w=["tmp32_%d" % tm])
                        other = b0
                    else:
                        P.op("act", lambda e, tm=tm, b1=b1, n=n: e.activation(out=tmp32[tm][:, 0:n], in_=ps[b1][:, 0:n], func=AF.Silu),
                             r=["ps%d" % b1], w=["tmp32_%d" % tm])
                        other = b0
                    P.op("dve", lambda e, dst=dst, other=other, tm=tm, n=n: e.tensor_tensor(
                        out=dst, in0=ps[other][:, 0:n], in1=tmp32[tm][:, 0:n], op=ALU.mult),
                        r=["ps%d" % other, "tmp32_%d" % tm], w=[skey])
                elif ut in ("rope_q", "rope_k"):
                    b = pbs[0]
                    if t0 >= NTOK:
                        P.op("act", lambda e, dst=dst, b=b, n=n: e.copy(out=dst, in_=ps[b][:, 0:n]), r=["ps%d" % b], w=[skey])
                    else:
                        tm = cnt["tmp"] % 3
                        cnt["tmp"] += 1
                        rp = cnt["rp"] % 2
                        cnt["rp"] += 1
                        P.op("act", lambda e, tm=tm, b=b, n=n: e.copy(out=tmp32[tm][:, 0:n], in_=ps[b][:, 0:n]),
                             r=["ps%d" % b], w=["tmp32_%d" % tm])
                        b2 = cnt["ps"] % 8
                        cnt["ps"] += 1
                        P.op("pe", lambda e, b2=b2, tm=tm, n=n: e.matmul(ps[b2][:, 0:n], lhsT=rope["perm"][:], rhs=tmp32[tm][:, 0:n],
                                                                        start=True, stop=True),
                             r=["tmp32_%d" % tm, "perm"], w=["ps%d" % b2])
                        P.op("dve", lambda e, rp=rp, tm=tm, t0=t0, n=n: e.tensor_tensor(
                            out=t1[rp][:, 0:n], in0=tmp32[tm][:, 0:n], in1=rope["C"][:, t0:t0 + n], op=ALU.mult),
                            r=["tmp32_%d" % tm, "ropeC"], w=["ropet1_%d" % rp])
                        P.op("dve", lambda e, rp=rp, b2=b2, t0=t0, n=n: e.tensor_tensor(
                            out=t2[rp][:, 0:n], in0=ps[b2][:, 0:n], in1=rope["S"][:, t0:t0 + n], op=ALU.mult),
                            r=["ps%d" % b2, "ropeS"], w=["ropet2_%d" % rp])
                        P.op("pool", lambda e, dst=dst, rp=rp, n=n: e.tensor_tensor(
                            out=dst, in0=t1[rp][:, 0:n], in1=t2[rp][:, 0:n], op=ALU.add),
                            r=["ropet1_%d" % rp, "ropet2_%d" % rp], w=[skey])
            if ut == "rope_q":
                P.dma(lambda e, stg=stg, oi=oi: e.dma_start(out=outs["qT"][oi], in_=stg[:]), r=[skey])
            elif ut == "rope_k":
                P.dma(lambda e, stg=stg, oi=oi: e.dma_start(out=outs["kT"][oi], in_=stg[:]), r=[skey])
            elif ut in ("silu", "mulsilu"):
                P.dma(lambda e, stg=stg, oi=oi: e.dma_start(out=outs["gate"][oi], in_=stg[:]), r=[skey])
            else:
                P.dma(lambda e, stg=stg, oi=oi: e.dma_start(out=outs["pre"][oi], in_=stg[:]), r=[skey])


def emit_phaseA(cx, li, din, outs):
    P = cx.P
    kind = KINDS[li]
    st = contextlib.ExitStack()
    pre = "l%d_" % li
    wslots = [P.sbuf("wblk%d" % i, [128, 16, 512], BF16, st) for i in range(2)]
    hT = P.sbuf("hT", [128, 16, T], BF16, st)
    st2 = contextlib.ExitStack()
    bc = emit_adaln(cx, st2, din[pre + "ada_w"], din[pre + "ada_b2"], din[pre + "norm_g2"], din["c2T"],
                    outs["grow"], wslots)
    emit_norm(cx, st2, din["xl"], hT, bc)
    P.barrier()
    st2.close()
    rope = None
    if kind == "attn":
        rope = {"C": P.sbuf("ropeC", [128, NTOK], F32, st), "S": P.sbuf("ropeS", [128, NTOK], F32, st),
                "perm": P.sbuf("perm", [128, 128], F32, st)}
        P.dma(lambda e: e.dma_start(out=rope["C"][:], in_=din["ropeC"]), w=["ropeC"])
        P.dma(lambda e: e.dma_start(out=rope["S"][:], in_=din["ropeS"]), w=["ropeS"])
        P.dma(lambda e: e.dma_start(out=rope["perm"][:], in_=din["perm"]), w=["perm"])
    emit_inproj(cx, st, kind, din[pre + "w_in"], hT, wslots, outs, rope)
    P.barrier()
    st.close()


def emit_attn(cx, li, din, uT_d):
    P = cx.P
    ps = cx.ps
    pre = "l%d_" % li
    keep_ctx = li != DEPTH - 1
    lam_init = 0.8 - 0.6 * math.exp(-0.3 * li)
    st = contextlib.ExitStack()
    qT_d, kT_d, V_d, gate_d = din["qT"], din["kT_all"], din["V_all"], din["gate"]
    NCH = NKEYS // 128
    kTh = [P.sbuf("kTh%d" % i, [128, NKEYS], BF16, st) for i in range(2)]
    Vh = [P.sbuf("Vh%d" % i, [128, NCH, 128], BF16, st) for i in range(2)]
    qTh = [P.sbuf("qTh%d" % i, [128, T], BF16, st) for i in range(2)]
    gh = [P.sbuf("gh%d" % i, [128, T], BF16, st) for i in range(2)]
    ust = [P.sbuf("ust%d" % i, [128, T], BF16, st) for i in range(2)]
    pT = [P.sbuf("pT%d" % i, [128, 512], BF16, st) for i in range(4)]
    rr = [P.sbuf("rr%d" % i, [128, 512], F32, st) for i in range(2)]
    oo = [P.sbuf("oo%d" % i, [128, 512], F32, st) for i in range(2)]
    osq = P.sbuf("osq", [128, 512], F32, st)
    rsb = P.sbuf("rsb", [128, 512], F32, st)
    lamv = P.sbuf("lamv", [1, 256], F32, st)
    lamt = P.sbuf("lamt", [1, 8], F32, st)
    lamj = P.sbuf("lamj", [1, 64], F32, st)
    neglam = P.sbuf("neglam", [128, 1], F32, st)
    hg = P.sbuf("hg", [128, 1], F32, st)
    P.dma(lambda e: e.dma_start(out=lamv[:], in_=din[pre + "lam"]), w=["lamv"])
    P.dma(lambda e: e.dma_start(out=hg[:], in_=din[pre + "head_g"]), w=["hg"])
    P.op("dve", lambda e: e.tensor_tensor(out=lamj[:], in0=lamv[:, 0:64], in1=lamv[:, 64:128], op=ALU.mult), r=["lamv"], w=["lamj"])
    P.op("dve", lambda e: e.reduce_sum(out=lamt[:, 0:1], in_=lamj[:], axis=mybir.AxisListType.X), r=["lamj"], w=["lamt0"])
    P.op("dve", lambda e: e.tensor_tensor(out=lamj[:], in0=lamv[:, 128:192], in1=lamv[:, 192:256], op=ALU.mult), r=["lamv", "lamt0"], w=["lamj"])
    P.op("dve", lambda e: e.reduce_sum(out=lamt[:, 1:2], in_=lamj[:], axis=mybir.AxisListType.X), r=["lamj"], w=["lamt1"])
    P.op("act", lambda e: e.activation(out=lamt[:, 2:4], in_=lamt[:, 0:2], func=AF.Exp), r=["lamt0", "lamt1"], w=["lamt2"])
    P.op("dve", lambda e: e.tensor_tensor(out=lamt[:, 4:5], in0=lamt[:, 3:4], in1=lamt[:, 2:3], op=ALU.subtract), r=["lamt2"], w=["lamt4"])
    P.op("dve", lambda e: e.tensor_scalar(out=lamt[:, 5:6], in0=lamt[:, 4:5], scalar1=-lam_init, scalar2=None, op0=ALU.add),
         r=["lamt4"], w=["lamt5"])
    P.op("pe", lambda e: e.matmul(ps[7][:, 0:1], lhsT=cx.onesf[0:1, :], rhs=lamt[0:1, 5:6], start=True, stop=True),
         r=["lamt5", "onesf"], w=["ps7"])
    P.op("dve", lambda e: e.tensor_copy(out=neglam[:], in_=ps[7][:, 0:1]), r=["ps7"], w=["neglam"])
    P.op("dve", lambda e: e.tensor_scalar(out=hg[:], in0=hg[:], scalar1=1.0 - lam_init, scalar2=None, op0=ALU.mult), r=["hg"], w=["hg"])

    qtiles = [(0, 512, 0, NCH), (512, 512, 0, NCH), (1024, 512, 0, NCH), (1536, 512, 0, NCH)]
    if keep_ctx:
        qtiles.append((NTOK, NCTX, 0, NCTX // 128))
    nq = T if keep_ctx else NTOK

    def load_head(h):
        s = h % 2
        P.dma(lambda e: e.dma_start(out=kTh[s][:], in_=kT_d[h]), w=["kTh%d" % s])
        P.dma(lambda e: e.dma_start(out=Vh[s][:], in_=V_d[h].rearrange("(c p) e -> p c e", p=128)), w=["Vh%d" % s])
        P.dma(lambda e: e.dma_start(out=qTh[s][:, 0:nq], in_=qT_d[h][:, 0:nq]), w=["qTh%d" % s])
        P.dma(lambda e: e.dma_start(out=gh[s][:, 0:nq], in_=gate_d[h][:, 0:nq]), w=["gh%d" % s])

    def S_op(s, c, kt, slot, q0, n):
        P.op("pe", lambda e: e.matmul(
            ps[slot][:, 0:n], lhsT=kTh[s][c * 64:(c + 1) * 64, kt * 128:(kt + 1) * 128],
            rhs=qTh[s][c * 64:(c + 1) * 64, q0:q0 + n], start=True, stop=True),
            r=["kTh%d" % s, "qTh%d" % s], w=["ps%d" % slot])

    def E_op(slot, pslot, n):
        P.op("act", lambda e: e.activation(out=pT[pslot][:, 0:n], in_=ps[slot][:, 0:n], func=AF.Exp, scale=0.125),
             r=["ps%d" % slot], w=["pT%d" % pslot])

    def PV_op(s, c, kt, pslot, first, last, n):
        po, pr = ps[3 + c], ps[5 + c]
        P.op("pe", lambda e: e.matmul(po[:, 0:n], lhsT=Vh[s][:, kt, :], rhs=pT[pslot][:, 0:n], start=first, stop=last),
             r=["Vh%d" % s, "pT%d" % pslot], w=["ps%d" % (3 + c)])
        P.op("pe", lambda e: e.matmul(pr[:, 0:n], lhsT=cx.onesb[:], rhs=pT[pslot][:, 0:n], start=first, stop=last),
             r=["onesb", "pT%d" % pslot], w=["ps%d" % (5 + c)])

    def norm_c(c, n):
        po, pr = ps[3 + c], ps[5 + c]
        P.op("dve", lambda e: e.reciprocal(out=rr[c][:, 0:n], in_=pr[:, 0:n]), r=["ps%d" % (5 + c)], w=["rr%d" % c])
        P.op("dve", lambda e: e.tensor_tensor(out=oo[c][:, 0:n], in0=po[:, 0:n], in1=rr[c][:, 0:n], op=ALU.mult),
             r=["ps%d" % (3 + c), "rr%d" % c], w=["oo%d" % c])

    def finish(s, q0, n):
        P.op("dve", lambda e: e.scalar_tensor_tensor(out=oo[0][:, 0:n], in0=oo[1][:, 0:n], scalar=neglam[:, 0:1],
                                                      in1=oo[0][:, 0:n], op0=ALU.mult, op1=ALU.add),
             r=["oo0", "oo1", "neglam"], w=["oo0"])
        P.op("act", lambda e: e.activation(out=osq[:, 0:n], in_=oo[0][:, 0:n], func=AF.Square), r=["oo0"], w=["osq"])
        P.op("pe", lambda e: e.matmul(ps[7][:, 0:n], lhsT=cx.onesf[:], rhs=osq[:, 0:n], start=True, stop=True),
             r=["osq", "onesf"], w=["ps7"])
        P.op("dve", lambda e: e.tensor_scalar(out=rsb[:, 0:n], in0=ps[7][:, 0:n], scalar1=1.0 / 128, scalar2=NORM_EPS,
                                               op0=ALU.mult, op1=ALU.add), r=["ps7"], w=["rsb"])
        P.op("act", lambda e: e.sqrt(out=rsb[:, 0:n], in_=rsb[:, 0:n]), r=["rsb"], w=["rsb"])
        P.op("dve", lambda e: e.reciprocal(out=rsb[:, 0:n], in_=rsb[:, 0:n]), r=["rsb"], w=["rsb"])
        P.op("dve", lambda e: e.scalar_tensor_tensor(out=oo[0][:, 0:n], in0=oo[0][:, 0:n], scalar=hg[:, 0:1], in1=rsb[:, 0:n],
                                                      op0=ALU.mult, op1=ALU.mult), r=["oo0", "hg", "rsb"], w=["oo0"])
        P.op("pool", lambda e: e.tensor_tensor(out=ust[s][:, q0:q0 + n], in0=oo[0][:, 0:n], in1=gh[s][:, q0:q0 + n], op=ALU.mult),
             r=["oo0", "gh%d" % s], w=["ust%d" % s])

    def store_head(h, s):
        P.dma(lambda e: e.dma_start(out=uT_d[h][:, 0:nq], in_=ust[s][:, 0:nq]), r=["ust%d" % s])

    load_head(0)
    cs = 0
    cp = 0
    for h in range(16):
        if h + 1 < 16:
            load_head(h + 1)
        s = h % 2
        for (q0, n, k0, k1) in qtiles:
            for c in range(2):
                kts = list(range(k0, k1))
                slots = {}
                for j in range(min(2, len(kts))):
                    slots[kts[j]] = cs % 3
                    S_op(s, c, kts[j], cs % 3, q0, n)
                    cs += 1
                for idx, kt in enumerate(kts):
                    pslot = cp % 4
                    cp += 1
                    E_op(slots[kt], pslot, n)
                    if idx + 2 < len(kts):
                        slots[kts[idx + 2]] = cs % 3
                        S_op(s, c, kts[idx + 2], cs % 3, q0, n)
                        cs += 1
                    PV_op(s, c, kt, pslot, idx == 0, idx == len(kts) - 1, n)
                norm_c(c, n)
            finish(s, q0, n)
        store_head(h, s)
    P.barrier()
    st.close()


def emit_conv(cx, li, din, uT_d):
    P = cx.P
    ps = cx.ps
    kind = KINDS[li]
    pre = "l%d_" % li
    st = contextlib.ExitStack()
    K = 31 if kind == "conf" else 3
    hal = K // 2
    prep_d, gate_d = din["pre_pad"], din["gate"]
    W = 512 + 2 * HAL
    pin = [P.sbuf("pin%d" % i, [128, 16, W], F32, st) for i in range(2)]
    gin = [P.sbuf("gin%d" % i, [128, 16, 512], BF16, st) for i in range(2)]
    uo = [P.sbuf("uo%d" % i, [128, 16, 512], BF16, st) for i in range(2)]
    cw = P.sbuf("cw", [128, 16, K], F32, st)
    P.dma(lambda e: e.dma_start(out=cw[:], in_=din[pre + "cw"]), w=["cw"])
    if kind == "conf":
        cc = P.sbuf("cc", [128, 16, 512], F32, st)
        accb = [P.sbuf("accb%d" % i, [128, 512], F32, st) for i in range(2)]
        csq = [P.sbuf("csq%d" % i, [128, 512], F32, st) for i in range(2)]
        ctmp = [P.sbuf("ctmp%d" % i, [128, 512], F32, st) for i in range(3)]
        mean = P.sbuf("mean", [128, 512], F32, st)
        msq = P.sbuf("msq", [128, 512], F32, st)
        rstd = P.sbuf("crstd", [128, 512], F32, st)
        nn = [P.sbuf("nn%d" % i, [128, 512], F32, st) for i in range(2)]
        sl = [P.sbuf("sl%d" % i, [128, 512], F32, st) for i in range(2)]
        vecs = P.sbuf("cvecs", [128, 3, 16], F32, st)
        P.dma(lambda e: e.dma_start(out=vecs[:], in_=din[pre + "cvecs"]), w=["cvecs"])
    else:
        yy = [P.sbuf("yy%d" % i, [128, 512], F32, st) for i in range(2)]
    NDVE = 17 if kind == "conf" else 3
    for ti, (t0, n) in enumerate(TT):
        s = ti % 2
        p0 = t0 if t0 < NTOK else CTXP
        P.dma(lambda e, s=s, p0=p0, n=n: e.dma_start(out=pin[s][:, :, 0:n + 2 * HAL],
                                                       in_=prep_d[:, :, p0:p0 + n + 2 * HAL].rearrange("j p t -> p j t")),
              w=["pin%d" % s])
        P.dma(lambda e, s=s, t0=t0, n=n: e.dma_start(out=gin[s][:, :, 0:n], in_=gate_d[:, :, t0:t0 + n].rearrange("j p t -> p j t")),
              w=["gin%d" % s])
        off = HAL - hal
        if kind == "conf":
            for j in range(16):
                acc = cc[:, j, 0:n]
                ab = accb[j % 2]
                P.op("dve", lambda e, acc=acc, j=j, s=s, n=n: e.tensor_scalar(
                    out=acc, in0=pin[s][:, j, off:off + n], scalar1=cw[:, j, 0:1], scalar2=vecs[:, 0, j:j + 1],
                    op0=ALU.mult, op1=ALU.add), r=["pin%d" % s, "cw", "cvecs"], w=["cc%d" % j])
                for k in range(1, NDVE):
                    P.op("dve", lambda e, acc=acc, j=j, s=s, n=n, k=k: e.scalar_tensor_tensor(
                        out=acc, in0=pin[s][:, j, off + k:off + k + n], scalar=cw[:, j, k:k + 1], in1=acc,
                        op0=ALU.mult, op1=ALU.add), r=["pin%d" % s, "cw", "cc%d" % j], w=["cc%d" % j])
                P.op("act", lambda e, ab=ab, j=j, s=s, n=n: e.activation(
                    out=ab[:, 0:n], in_=pin[s][:, j, off + NDVE:off + NDVE + n], func=AF.Identity, scale=cw[:, j, NDVE:NDVE + 1]),
                    r=["pin%d" % s, "cw"], w=["accb%d" % (j % 2)])
                for k in range(NDVE + 1, K):
                    tq = k % 3
                    P.op("act", lambda e, tq=tq, j=j, s=s, n=n, k=k: e.activation(
                        out=ctmp[tq][:, 0:n], in_=pin[s][:, j, off + k:off + k + n], func=AF.Identity, scale=cw[:, j, k:k + 1]),
                        r=["pin%d" % s, "cw"], w=["ctmp%d" % tq])
                    P.op("pool", lambda e, ab=ab, tq=tq, n=n: e.tensor_tensor(
                        out=ab[:, 0:n], in0=ab[:, 0:n], in1=ctmp[tq][:, 0:n], op=ALU.add),
                        r=["ctmp%d" % tq, "accb%d" % (j % 2)], w=["accb%d" % (j % 2)])
                P.op("dve", lambda e, acc=acc, ab=ab, n=n: e.tensor_tensor(out=acc, in0=acc, in1=ab[:, 0:n], op=ALU.add),
                     r=["cc%d" % j, "accb%d" % (j % 2)], w=["cc%d" % j])
                q = j % 2
                P.op("act", lambda e, acc=acc, q=q, n=n: e.activation(out=csq[q][:, 0:n], in_=acc, func=AF.Square),
                     r=["cc%d" % j], w=["csq%d" % q])
                P.op("pe", lambda e, acc=acc, j=j, n=n: e.matmul(ps[0][:, 0:n], lhsT=cx.onesf[:], rhs=acc, start=(j == 0), stop=(j == 15)),
                     r=["cc%d" % j, "onesf"], w=["ps0"])
                P.op("pe", lambda e, q=q, j=j, n=n: e.matmul(ps[1][:, 0:n], lhsT=cx.onesf[:], rhs=csq[q][:, 0:n], start=(j == 0), stop=(j == 15)),
                     r=["csq%d" % q, "onesf"], w=["ps1"])
            P.op("dve", lambda e, n=n: e.tensor_scalar(out=mean[:, 0:n], in0=ps[0][:, 0:n], scalar1=1.0 / D, scalar2=None, op0=ALU.mult),
                 r=["ps0"], w=["mean"])
            P.op("dve", lambda e, n=n: e.tensor_tensor(out=msq[:, 0:n], in0=mean[:, 0:n], in1=mean[:, 0:n], op=ALU.mult), r=["mean"], w=["msq"])
            P.op("dve", lambda e, n=n: e.scalar_tensor_tensor(out=rstd[:, 0:n], in0=ps[1][:, 0:n], scalar=1.0 / D, in1=msq[:, 0:n],
                                                               op0=ALU.mult, op1=ALU.subtract), r=["ps1", "msq"], w=["crstd"])
            P.op("dve", lambda e, n=n: e.tensor_scalar(out=rstd[:, 0:n], in0=rstd[:, 0:n], scalar1=LN_EPS, scalar2=None, op0=ALU.add),
                 r=["crstd"], w=["crstd"])
            P.op("act", lambda e, n=n: e.sqrt(out=rstd[:, 0:n], in_=rstd[:, 0:n]), r=["crstd"], w=["crstd"])
            P.op("dve", lambda e, n=n: e.reciprocal(out=rstd[:, 0:n], in_=rstd[:, 0:n]), r=["crstd"], w=["crstd"])
            for j in range(16):
                q = j % 2
                acc = cc[:, j, 0:n]
                P.op("dve", lambda e, acc=acc, q=q, n=n: e.tensor_tensor(out=nn[q][:, 0:n], in0=acc, in1=mean[:, 0:n], op=ALU.subtract),
                     r=["cc%d" % j, "mean"], w=["nn%d" % q])
                P.op("dve", lambda e, q=q, n=n: e.tensor_tensor(out=nn[q][:, 0:n], in0=nn[q][:, 0:n], in1=rstd[:, 0:n], op=ALU.mult),
                     r=["nn%d" % q, "crstd"], w=["nn%d" % q])
                P.op("act", lambda e, q=q, j=j, n=n: e.activation(out=sl[q][:, 0:n], in_=nn[q][:, 0:n], func=AF.Silu,
                                                                 scale=vecs[:, 1, j:j + 1], bias=vecs[:, 2, j:j + 1]),
                     r=["nn%d" % q, "cvecs"], w=["sl%d" % q])
                P.op("pool", lambda e, q=q, j=j, s=s, n=n: e.tensor_tensor(out=uo[s][:, j, 0:n], in0=sl[q][:, 0:n], in1=gin[s][:, j, 0:n], op=ALU.mult),
                     r=["sl%d" % q, "gin%d" % s], w=["uo%d" % s])
        else:
            for j in range(16):
                q = j % 2
                P.op("dve", lambda e, q=q, j=j, s=s, n=n: e.tensor_scalar(
                    out=yy[q][:, 0:n], in0=pin[s][:, j, off:off + n], scalar1=cw[:, j, 0:1], scalar2=None, op0=ALU.mult),
                    r=["pin%d" % s, "cw"], w=["yy%d" % q])
                for k in range(1, K):
                    P.op("dve", lambda e, q=q, j=j, s=s, n=n, k=k: e.scalar_tensor_tensor(
                        out=yy[q][:, 0:n], in0=pin[s][:, j, off + k:off + k + n], scalar=cw[:, j, k:k + 1], in1=yy[q][:, 0:n],
                        op0=ALU.mult, op1=ALU.add), r=["pin%d" % s, "cw", "yy%d" % q], w=["yy%d" % q])
                P.op("pool", lambda e, q=q, j=j, s=s, n=n: e.tensor_tensor(out=uo[s][:, j, 0:n], in0=yy[q][:, 0:n], in1=gin[s][:, j, 0:n], op=ALU.mult),
                     r=["yy%d" % q, "gin%d" % s], w=["uo%d" % s])
        P.dma(lambda e, s=s, t0=t0, n=n: e.dma_start(out=uT_d[:, :, t0:t0 + n].rearrange("j p t -> p j t"), in_=uo[s][:, :, 0:n]),
              r=["uo%d" % s])
    P.barrier()
    st.close()


def emit_outproj(cx, li, din, uT_d, xl_in, xl_out, final):
    P = cx.P
    ps = cx.ps
    pre = "l%d_" % li
    st = contextlib.ExitStack()
    last = li == DEPTH - 1
    wo = P.sbuf("wo", [128, 16, D], BF16, st)
    wv = din[pre + "w_out"].rearrange("(kc p) n -> p kc n", p=128)
    for n in range(4):
        P.dma(lambda e, n=n: e.dma_start(out=wo[:, :, n * 512:(n + 1) * 512], in_=wv[:, :, n * 512:(n + 1) * 512]),
              w=["wo%d" % n], q="pool")
    g2 = P.sbuf("g2", [2, D], F32, st)
    P.dma(lambda e: e.dma_start(out=g2[:], in_=din["grow"]), w=["g2"])
    gbc = []
    k = 0
    for row in (0, 1):
        bc = P.sbuf("gbc%d" % row, [128, D], F32, st)
        for n in range(4):
            b = k % 2
            k += 1
            P.op("pe", lambda e, b=b, row=row, n=n: e.matmul(ps[b][:], lhsT=cx.sel[0:2, row * 128:(row + 1) * 128],
                                                            rhs=g2[0:2, n * 512:(n + 1) * 512], start=True, stop=True),
                 r=["g2", "sel"], w=["ps%d" % b])
            P.op("act", lambda e, b=b, bc=bc, n=n: e.copy(out=bc[:, n * 512:(n + 1) * 512], in_=ps[b][:]), r=["ps%d" % b], w=["gbc%d" % row])
        gbc.append(bc)
    if final:
        fg = P.sbuf("fg", [128, D], F32, st)
        P.dma(lambda e: e.dma_start(out=fg[:], in_=din["final_g_bc"]), w=["fg"])
        fss = P.sbuf("fss", [128, 2], F32, st)
        fjunk = P.sbuf("fjunk", [128, D], BF16, st)
    ntile = (NTOK if last else T) // 128
    ut = [P.sbuf("ut%d" % i, [128, 16, 128], BF16, st) for i in range(3)]
    xt = [P.sbuf("oxt%d" % i, [128, D], F32, st) for i in range(3)]
    tmp = [P.sbuf("otmp%d" % i, [128, 512], F32, st) for i in range(2)]
    cb = 0
    for i in range(ntile):
        s = i % 3
        row = 0 if i < NTOK // 128 else 1
        P.dma(lambda e, s=s, i=i: e.dma_start(out=ut[s][:], in_=uT_d[:, :, i * 128:(i + 1) * 128].rearrange("j p t -> p j t")),
              w=["ut%d" % s])
        P.dma(lambda e, s=s, i=i: e.dma_start(out=xt[s][:], in_=xl_in[i * 128:(i + 1) * 128, :]), w=["oxt%d" % s])
        for n in range(4):
            b = cb % 8
            cb += 1

            def mm(e, b=b, s=s, n=n):
                for kc in range(16):
                    ins = e.matmul(ps[b][:], lhsT=ut[s][:, kc, :], rhs=wo[:, kc, n * 512:(n + 1) * 512], start=(kc == 0), stop=(kc == 15))
                return ins
            P.op("pe", mm, r=["ut%d" % s, "wo%d" % n], w=["ps%d" % b])
            tq = cb % 2
            P.op("dve", lambda e, b=b, tq=tq, row=row, n=n: e.tensor_tensor(out=tmp[tq][:], in0=ps[b][:], in1=gbc[row][:, n * 512:(n + 1) * 512], op=ALU.mult),
                 r=["ps%d" % b, "gbc%d" % row], w=["otmp%d" % tq])
            P.op("pool", lambda e, tq=tq, s=s, n=n: e.tensor_tensor(out=xt[s][:, n * 512:(n + 1) * 512], in0=tmp[tq][:], in1=xt[s][:, n * 512:(n + 1) * 512], op=ALU.add),
                 r=["otmp%d" % tq, "oxt%d" % s], w=["oxt%d" % s])
        if final:
            c = i % 2
            P.op("act", lambda e, s=s, c=c: e.activation(out=fjunk[:], in_=xt[s][:], func=AF.Square, accum_out=fss[:, c:c + 1]),
                 r=["oxt%d" % s], w=["fjunk", "fss%d" % c])
            P.op("dve", lambda e, c=c: e.tensor_scalar(out=fss[:, c:c + 1], in0=fss[:, c:c + 1], scalar1=1.0 / D, scalar2=NORM_EPS,
                                                        op0=ALU.mult, op1=ALU.add), r=["fss%d" % c], w=["fss%d" % c])
            P.op("act", lambda e, c=c: e.sqrt(out=fss[:, c:c + 1], in_=fss[:, c:c + 1]), r=["fss%d" % c], w=["fss%d" % c])
            P.op("dve", lambda e, c=c: e.reciprocal(out=fss[:, c:c + 1], in_=fss[:, c:c + 1]), r=["fss%d" % c], w=["fss%d" % c])
            P.op("dve", lambda e, s=s, c=c: e.scalar_tensor_tensor(out=xt[s][:], in0=xt[s][:], scalar=fss[:, c:c + 1], in1=fg[:],
                                                                    op0=ALU.mult, op1=ALU.mult), r=["oxt%d" % s, "fss%d" % c, "fg"], w=["oxt%d" % s])
        P.dma(lambda e, s=s, i=i: e.dma_start(out=xl_out[i * 128:(i + 1) * 128, :], in_=xt[s][:]), r=["oxt%d" % s],
              is_out=True)
    P.barrier()
    st.close()


def _bf16(a):
    return a.astype(ml_dtypes.bfloat16)


def rope_tables(s):
    tg = np.arange(NTOK, dtype=np.int64) + s * NTOK
    row = (tg // 64).astype(np.float32)
    col = (tg % 64).astype(np.float32)
    inv = (np.float32(10000.0) ** (-np.arange(16, dtype=np.float32) / np.float32(16))).astype(np.float32)
    C = np.zeros((128, NTOK), np.float32)
    S = np.zeros((128, NTOK), np.float32)
    for p in range(128):
        idx = p % 64
        axis, half, f = idx // 32, (idx // 16) % 2, idx % 16
        pos = row if axis == 0 else col
        ang = (pos * inv[f]).astype(np.float32)
        C[p] = np.cos(ang)
        S[p] = np.sin(ang) * (-1.0 if half == 0 else 1.0)
    return C, S


def perm_matrix():
    Pm = np.zeros((128, 128), np.float32)
    for m in range(128):
        half = ((m % 64) // 16) % 2
        k = m + 16 if half == 0 else m - 16
        Pm[k, m] = 1.0
    return Pm


def sel_matrix():
    s = np.zeros((2, 256), np.float32)
    s[0, 0:128] = 1.0
    s[1, 128:256] = 1.0
    return s


_NC_CACHE = {}
_dbg = None


def _dram_in(nc, name, shape, dt):
    return nc.dram_tensor(name, list(shape), dt, kind="ExternalInput").ap()


def _dram_out(nc, name, shape, dt):
    return nc.dram_tensor(name, list(shape), dt, kind="ExternalOutput").ap()


def build_A(li):
    kind = KINDS[li]
    pre = "l%d_" % li
    nc = bass.Bass("TRN2", target_bir_lowering=False)
    din = {}
    for nm, shp in (("ident", [128, 128]), ("sel", [2, 256]), ("c2T", [128, 16, 2]), ("xl", [T, D]),
                    (pre + "ada_w", [D, 3 * D]), (pre + "ada_b2", [2, 3 * D]), (pre + "norm_g2", [2, D]),
                    (pre + "w_in", [D, NCOLS[li]])):
        din[nm] = _dram_in(nc, nm, shp, F32)
    outs = {"grow": _dram_out(nc, "grow", [2, D], F32)}
    if kind == "attn":
        for nm, shp in (("ropeC", [128, NTOK]), ("ropeS", [128, NTOK]), ("perm", [128, 128])):
            din[nm] = _dram_in(nc, nm, shp, F32)
        outs["qT"] = _dram_out(nc, "qT", [16, 128, T], BF16)
        outs["kT"] = _dram_out(nc, "kT", [16, 128, T], BF16)
        outs["V"] = _dram_out(nc, "V", [16, T, 128], BF16)
        outs["gate"] = _dram_out(nc, "gate", [16, 128, T], BF16)
    else:
        outs["pre"] = _dram_out(nc, "pre", [16, 128, T], F32)
        outs["gate"] = _dram_out(nc, "gate", [16, 128, T], BF16)
    P = Prog(nc)
    cx = setup_consts(P, din)
    emit_phaseA(cx, li, din, outs)
    P.out_dma_ops = P.dma_ops["sp"][-NDMASEM:] + P.dma_ops["pool"][-NDMASEM:]
    P.build()
    return nc


def build_BC(li):
    kind = KINDS[li]
    pre = "l%d_" % li
    last = li == DEPTH - 1
    nc = bass.Bass("TRN2", target_bir_lowering=False)
    din = {}
    for nm, shp in (("ident", [128, 128]), ("sel", [2, 256]), ("xl", [T, D]), ("grow", [2, D]), (pre + "w_out", [D, D])):
        din[nm] = _dram_in(nc, nm, shp, F32)
    din["gate"] = _dram_in(nc, "gate", [16, 128, T], BF16)
    if kind == "attn":
        din["qT"] = _dram_in(nc, "qT", [16, 128, T], BF16)
        din["kT_all"] = _dram_in(nc, "kT_all", [16, 128, NKEYS], BF16)
        din["V_all"] = _dram_in(nc, "V_all", [16, NKEYS, 128], BF16)
        din[pre + "lam"] = _dram_in(nc, pre + "lam", [1, 256], F32)
        din[pre + "head_g"] = _dram_in(nc, pre + "head_g", [128, 1], F32)
    else:
        K = 31 if kind == "conf" else 3
        din["pre_pad"] = _dram_in(nc, "pre_pad", [16, 128, TP], F32)
        din[pre + "cw"] = _dram_in(nc, pre + "cw", [128, 16, K], F32)
        if kind == "conf":
            din[pre + "cvecs"] = _dram_in(nc, pre + "cvecs", [128, 3, 16], F32)
    if last:
        din["final_g_bc"] = _dram_in(nc, "final_g_bc", [128, D], F32)
    uT_d = nc.dram_tensor("uT_scr", [16, 128, T], BF16).ap()
    nrow = NTOK if last else T
    xl_out = _dram_out(nc, "xl_out", [nrow, D], F32)
    P = Prog(nc)
    cx = setup_consts(P, din)
    if kind == "attn":
        emit_attn(cx, li, din, uT_d)
    else:
        emit_conv(cx, li, din, uT_d)
    emit_outproj(cx, li, din, uT_d, din["xl"], xl_out, final=last)
    P.build()
    return nc


def get_nc(tag, li):
    key = (tag, li)
    if key not in _NC_CACHE:
        _NC_CACHE[key] = build_A(li) if tag == "A" else build_BC(li)
    return _NC_CACHE[key]


def kernel(**inp):
    inp = {k: np.asarray(v) for k, v in inp.items()}
    x, c, ctx, c_ctx = inp["x"], inp["c"], inp["ctx"], inp["c_ctx"]
    cores = list(range(NCORES))
    ident = np.eye(128, dtype=np.float32)
    sel = sel_matrix()
    perm = perm_matrix()
    ropes = [rope_tables(s) for s in range(2)]
    xl = []
    c2T = []
    for r in cores:
        b, s = r // 2, r % 2
        xl.append(np.ascontiguousarray(np.concatenate([x[b, s * NTOK:(s + 1) * NTOK], ctx[b]], axis=0)))
        c2 = np.stack([c[b], c_ctx], axis=0)
        c2T.append(np.ascontiguousarray(c2.reshape(2, 16, 128).transpose(2, 1, 0)))
    for li in range(DEPTH):
        kind = KINDS[li]
        pre = "l%d_" % li
        last = li == DEPTH - 1
        ada_b2 = np.ascontiguousarray(np.broadcast_to(inp[pre + "ada_b"][None, :], (2, 3 * D)))
        norm_g2 = np.ascontiguousarray(np.broadcast_to(inp[pre + "norm_g"][None, :], (2, D)))
        ncA = get_nc("A", li)
        maps = []
        for r in cores:
            m = {"ident": ident, "sel": sel, "c2T": c2T[r], "xl": xl[r], pre + "ada_w": inp[pre + "ada_w"],
                 pre + "ada_b2": ada_b2, pre + "norm_g2": norm_g2, pre + "w_in": inp[pre + "w_in"]}
            if kind == "attn":
                m["ropeC"], m["ropeS"] = ropes[r % 2]
                m["perm"] = perm
            maps.append(m)
        resA = run_bass_kernel_spmd(ncA, maps, core_ids=cores).results
        ncB = get_nc("BC", li)
        maps = []
        for r in cores:
            b, s = r // 2, r % 2
            r0, r1 = 2 * b, 2 * b + 1
            m = {"ident": ident, "sel": sel, "xl": xl[r], "grow": resA[r]["grow"], pre + "w_out": inp[pre + "w_out"],
                 "gate": resA[r]["gate"]}
            if kind == "attn":
                m["qT"] = resA[r]["qT"]
                m["kT_all"] = np.ascontiguousarray(np.concatenate(
                    [resA[r]["kT"][:, :, NTOK:], resA[r0]["kT"][:, :, :NTOK], resA[r1]["kT"][:, :, :NTOK]], axis=2))
                m["V_all"] = np.ascontiguousarray(np.concatenate(
                    [resA[r]["V"][:, NTOK:, :], resA[r0]["V"][:, :NTOK, :], resA[r1]["V"][:, :NTOK, :]], axis=1))
                m[pre + "lam"] = np.concatenate([inp[pre + "lam_q1"], inp[pre + "lam_k1"], inp[pre + "lam_q2"],
                                                 inp[pre + "lam_k2"]])[None, :].astype(np.float32)
                m[pre + "head_g"] = np.ascontiguousarray(inp[pre + "head_g"][:, None])
            else:
                own = resA[r]["pre"]
                other = resA[r ^ 1]["pre"]
                pp = np.zeros((16, 128, TP), np.float32)
                pp[:, :, HAL:HAL + NTOK] = own[:, :, :NTOK]
                if s == 1:
                    pp[:, :, 0:HAL] = other[:, :, NTOK - HAL:NTOK]
                else:
                    pp[:, :, HAL + NTOK:2 * HAL + NTOK] = other[:, :, 0:HAL]
                pp[:, :, CTXP + HAL:CTXP + HAL + NCTX] = own[:, :, NTOK:]
                m["pre_pad"] = pp
                if kind == "conf":
                    m[pre + "cw"] = np.ascontiguousarray(inp[pre + "dw_w"].reshape(31, 16, 128).transpose(2, 1, 0))
                    m[pre + "cvecs"] = np.ascontiguousarray(np.stack(
                        [inp[pre + "dw_b"].reshape(16, 128).T, inp[pre + "ln_g"].reshape(16, 128).T,
                         inp[pre + "ln_b"].reshape(16, 128).T], axis=1))
                else:
                    m[pre + "cw"] = np.ascontiguousarray(inp[pre + "conv_w"].reshape(3, 16, 128).transpose(2, 1, 0))
            if last:
                m["final_g_bc"] = np.ascontiguousarray(np.broadcast_to(inp["final_norm_g"][None, :], (128, D)))
            maps.append(m)
        resB = run_bass_kernel_spmd(ncB, maps, core_ids=cores).results
        xl = [resB[r]["xl_out"] for r in cores]
        if _dbg is not None:
            _dbg(li, xl, resA, resB)
    out = np.empty((4, 4096, D), np.float32)
    for r in cores:
        b, s = r // 2, r % 2
        out[b, s * NTOK:(s + 1) * NTOK] = xl[r][:NTOK]
    return out
```
